# Optimizing a Trainium2 kernel written in Bass

```python
import jax, jax.numpy as jnp
from jax import lax
import numpy as np

D_MODEL = 2048
BATCH = 8
SEQ = 2048
DEPTH = 2

N_A_LAYERS = DEPTH // 2
N_B_LAYERS = DEPTH - N_A_LAYERS
N_META = 16
D_FF = 5504
ROPE_THETA = 500000.0
ROPE_FRACTION = 4
EPS = 1e-6

A_HEADS = 16
A_KV_HEADS = 4
A_HEAD_DIM = D_MODEL // A_HEADS
IDX_HEADS = 16
IDX_DIM = 64
TOPK_MAX = 256
A_QBLOCK = 64
A_SPLIT_SIZES = (A_HEADS * A_HEAD_DIM, A_KV_HEADS * A_HEAD_DIM, A_KV_HEADS * A_HEAD_DIM, IDX_HEADS * IDX_DIM, IDX_DIM)
A_IN_DIM = sum(A_SPLIT_SIZES) + IDX_HEADS

B_HEADS = 32
B_KV_HEADS = 4
B_HEAD_DIM = D_MODEL // B_HEADS
WINDOW = 128
BLOCK = 128

kernel_name = "yoco_dsa_swa_sink_macaron_hybrid"


def rms_norm(x, g):
    xf = x.astype(jnp.float32)
    y = xf * lax.rsqrt(jnp.mean(xf * xf, axis=-1, keepdims=True) + EPS)
    return (y * g.astype(jnp.float32)).astype(x.dtype)


def partial_rope(x, pos):
    dh = x.shape[-1]
    rot = dh // ROPE_FRACTION
    half = rot // 2
    inv = ROPE_THETA ** (-jnp.arange(half, dtype=jnp.float32) / half)
    ang = pos.astype(jnp.float32)[:, None] * inv[None, :]
    cos = jnp.cos(ang)[:, None, :]
    sin = jnp.sin(ang)[:, None, :]
    xf = x.astype(jnp.float32)
    x1 = xf[..., :half]
    x2 = xf[..., half:rot]
    out = jnp.concatenate([x1 * cos - x2 * sin, x2 * cos + x1 * sin, xf[..., rot:]], axis=-1)
    return out.astype(x.dtype)


def swiglu_half(h, g, w_gate, w_up, w_down):
    n = rms_norm(h, g)
    return h + 0.5 * ((jax.nn.silu(n @ w_gate) * (n @ w_up)) @ w_down)


def indexer_sparse_attention(hn, pos, w_in, q_norm, k_norm, idx_k_norm, w_out, topk):
    bsz, t_len, _ = hn.shape
    proj = hn @ w_in
    q, k, v, qi, ki, wi = jnp.split(proj, np.cumsum(A_SPLIT_SIZES).tolist(), axis=-1)
    q = partial_rope(rms_norm(q.reshape(bsz, t_len, A_HEADS, A_HEAD_DIM), q_norm), pos)
    k = partial_rope(rms_norm(k.reshape(bsz, t_len, A_KV_HEADS, A_HEAD_DIM), k_norm), pos)
    v = v.reshape(bsz, t_len, A_KV_HEADS, A_HEAD_DIM)
    qi = partial_rope(qi.reshape(bsz, t_len, IDX_HEADS, IDX_DIM), pos)
    ki = partial_rope(rms_norm(ki, idx_k_norm)[:, :, None, :], pos)[:, :, 0, :].astype(jnp.float32)
    wi = wi.astype(jnp.float32) * (IDX_HEADS * IDX_DIM) ** -0.5
    group = A_HEADS // A_KV_HEADS
    scale = A_HEAD_DIM ** -0.5
    nb = t_len // A_QBLOCK

    def to_blocks(a):
        return jnp.moveaxis(a.reshape(bsz, nb, A_QBLOCK, *a.shape[2:]), 1, 0)

    def block(args):
        qb, qib, wib, qp = args
        s_idx = jnp.einsum('bqhd,bsd->bqhs', qib.astype(jnp.float32), ki)
        score = jnp.einsum('bqhs,bqh->bqs', jax.nn.relu(s_idx), wib)
        causal = pos[None, :] <= qp[:, None]
        score = jnp.where(causal[None], score, -jnp.inf)
        _, sel = lax.top_k(score, topk)
        kg = jax.vmap(lambda kb, ib: kb[ib])(k, sel)
        vg = jax.vmap(lambda vb, ib: vb[ib])(v, sel)
        qg = qb.reshape(bsz, A_QBLOCK, A_KV_HEADS, group, A_HEAD_DIM)
        logits = jnp.einsum('bqngd,bqknd->bqngk', qg, kg).astype(jnp.float32) * scale
        valid = pos[sel] <= qp[None, :, None]
        logits = jnp.where(valid[:, :, None, None, :], logits, -jnp.inf)
        p = jax.nn.softmax(logits, axis=-1).astype(vg.dtype)
        o = jnp.einsum('bqngk,bqknd->bqngd', p, vg)
        return o.reshape(bsz, A_QBLOCK, A_HEADS * A_HEAD_DIM)

    out = lax.map(block, (to_blocks(q), to_blocks(qi), to_blocks(wi), pos.reshape(nb, A_QBLOCK)))
    out = jnp.moveaxis(out, 0, 1).reshape(bsz, t_len, A_HEADS * A_HEAD_DIM)
    return out @ w_out


def shared_kv(hn, pos, w_kv, k_norm):
    bsz, t_len, _ = hn.shape
    k, v = jnp.split(hn @ w_kv, 2, axis=-1)
    k = partial_rope(rms_norm(k.reshape(bsz, t_len, B_KV_HEADS, B_HEAD_DIM), k_norm), pos)
    v = v.reshape(bsz, t_len, B_KV_HEADS, B_HEAD_DIM)
    return k, v


def sliding_window_sink_attention(hn, pos, k, v, w_q, q_norm, sinks, w_out):
    bsz, t_len, _ = hn.shape
    group = B_HEADS // B_KV_HEADS
    scale = B_HEAD_DIM ** -0.5
    nb = t_len // BLOCK
    q = partial_rope(rms_norm((hn @ w_q).reshape(bsz, t_len, B_HEADS, B_HEAD_DIM), q_norm), pos)
    qb = jnp.moveaxis(q.reshape(bsz, nb, BLOCK, B_KV_HEADS, group, B_HEAD_DIM), 1, 0)
    kb = k.reshape(bsz, nb, BLOCK, B_KV_HEADS, B_HEAD_DIM)
    vb = v.reshape(bsz, nb, BLOCK, B_KV_HEADS, B_HEAD_DIM)
    pad = ((0, 0), (1, 0), (0, 0), (0, 0), (0, 0))
    kw = jnp.moveaxis(jnp.concatenate([jnp.pad(kb[:, :-1], pad), kb], axis=2), 1, 0)
    vw = jnp.moveaxis(jnp.concatenate([jnp.pad(vb[:, :-1], pad), vb], axis=2), 1, 0)
    c = jnp.arange(nb)[:, None, None]
    r = jnp.arange(BLOCK)[None, :, None]
    j = jnp.arange(2 * BLOCK)[None, None, :]
    rel = BLOCK + r - j
    mask = (rel >= 0) & (rel < WINDOW) & ((c > 0) | (j >= BLOCK))
    sink = sinks.astype(jnp.float32).reshape(1, B_KV_HEADS, group, 1, 1)

    def block(args):
        qc, kc, vc, mc = args
        logits = jnp.einsum('bqngd,bknd->bngqk', qc, kc).astype(jnp.float32) * scale
        logits = jnp.where(mc[None, None, None], logits, -jnp.inf)
        m = jnp.maximum(jnp.max(logits, axis=-1, keepdims=True), sink)
        e = jnp.exp(logits - m)
        p = e / (jnp.sum(e, axis=-1, keepdims=True) + jnp.exp(sink - m))
        return jnp.einsum('bngqk,bknd->bqngd', p.astype(vc.dtype), vc)

    out = lax.map(block, (qb, kw, vw, mask))
    out = jnp.moveaxis(out, 0, 1).reshape(bsz, t_len, B_HEADS * B_HEAD_DIM)
    return out @ w_out


def setup_inputs(seed: int = 0) -> dict:
    key = jax.random.key(seed)
    ks = iter(jax.random.split(key, 40))

    def nrm(shape, scale):
        return jax.random.normal(next(ks), shape, jnp.float32) * scale

    def gain(shape):
        return 1.0 + 0.02 * jax.random.normal(next(ks), shape, jnp.float32)

    d = D_MODEL
    return {
        "x": nrm((BATCH, SEQ, d), 1.0),
        "meta_tokens": nrm((N_META, d), 1.0),
        "ffn1_norm": gain((DEPTH, d)),
        "ffn1_w_gate": nrm((DEPTH, d, D_FF), d ** -0.5),
        "ffn1_w_up": nrm((DEPTH, d, D_FF), d ** -0.5),
        "ffn1_w_down": nrm((DEPTH, D_FF, d), D_FF ** -0.5),
        "ffn2_norm": gain((DEPTH, d)),
        "ffn2_w_gate": nrm((DEPTH, d, D_FF), d ** -0.5),
        "ffn2_w_up": nrm((DEPTH, d, D_FF), d ** -0.5),
        "ffn2_w_down": nrm((DEPTH, D_FF, d), D_FF ** -0.5),
        "a_norm": gain((N_A_LAYERS, d)),
        "a_w_in": nrm((N_A_LAYERS, d, A_IN_DIM), d ** -0.5),
        "a_q_norm": gain((N_A_LAYERS, A_HEAD_DIM)),
        "a_k_norm": gain((N_A_LAYERS, A_HEAD_DIM)),
        "a_idx_k_norm": gain((N_A_LAYERS, IDX_DIM)),
        "a_w_out": nrm((N_A_LAYERS, A_HEADS * A_HEAD_DIM, d), (A_HEADS * A_HEAD_DIM) ** -0.5),
        "kv_norm": gain((d,)),
        "kv_w": nrm((d, 2 * B_KV_HEADS * B_HEAD_DIM), d ** -0.5),
        "kv_k_norm": gain((B_HEAD_DIM,)),
        "b_norm": gain((N_B_LAYERS, d)),
        "b_w_q": nrm((N_B_LAYERS, d, B_HEADS * B_HEAD_DIM), d ** -0.5),
        "b_q_norm": gain((N_B_LAYERS, B_HEAD_DIM)),
        "b_sinks": nrm((N_B_LAYERS, B_HEADS), 1.0),
        "b_w_out": nrm((N_B_LAYERS, B_HEADS * B_HEAD_DIM, d), (B_HEADS * B_HEAD_DIM) ** -0.5),
    }


def reference(x, meta_tokens, ffn1_norm, ffn1_w_gate, ffn1_w_up, ffn1_w_down, ffn2_norm, ffn2_w_gate, ffn2_w_up, ffn2_w_down, a_norm, a_w_in, a_q_norm, a_k_norm, a_idx_k_norm, a_w_out, kv_norm, kv_w, kv_k_norm, b_norm, b_w_q, b_q_norm, b_sinks, b_w_out):
    bsz, s_len, _ = x.shape
    topk = min(TOPK_MAX, s_len // 4)
    t_real = s_len + N_META
    t_len = -(-t_real // BLOCK) * BLOCK
    meta = jnp.broadcast_to(meta_tokens[None].astype(x.dtype), (bsz, N_META, D_MODEL))
    h = jnp.concatenate([meta, x, jnp.zeros((bsz, t_len - t_real, D_MODEL), x.dtype)], axis=1)
    pos = jnp.arange(t_len, dtype=jnp.int32)
    k_sh = v_sh = None
    for layer in range(DEPTH):
        if layer == N_A_LAYERS:
            k_sh, v_sh = shared_kv(rms_norm(h, kv_norm), pos, kv_w, kv_k_norm)
        h = swiglu_half(h, ffn1_norm[layer], ffn1_w_gate[layer], ffn1_w_up[layer], ffn1_w_down[layer])
        if layer < N_A_LAYERS:
            i = layer
            h = h + indexer_sparse_attention(rms_norm(h, a_norm[i]), pos, a_w_in[i], a_q_norm[i], a_k_norm[i], a_idx_k_norm[i], a_w_out[i], topk)
        else:
            i = layer - N_A_LAYERS
            h = h + sliding_window_sink_attention(rms_norm(h, b_norm[i]), pos, k_sh, v_sh, b_w_q[i], b_q_norm[i], b_sinks[i], b_w_out[i])
        h = swiglu_half(h, ffn2_norm[layer], ffn2_w_gate[layer], ffn2_w_up[layer], ffn2_w_down[layer])
    return h[:, N_META:N_META + s_len]
```

```python
import math
from contextlib import ExitStack

import numpy as np
import concourse.bass as bass
import concourse.mybir as mybir
from concourse.bass_utils import run_bass_kernel_spmd

F32 = mybir.dt.float32
BF16 = mybir.dt.bfloat16
AF = mybir.ActivationFunctionType
ALU = mybir.AluOpType
AX = mybir.AxisListType

D = 2048
KD = 16
FF = 5504
KF = 43
NMETA = 16
SEQ = 2048
T = SEQ + NMETA
TP = 2176
NT = 17
TC = 344
NCH = 6
EPS = 1e-6
FGROUPS = [11, 11, 11, 10]
A_IN = 4176
A_IN_PAD = 4224
TOPK = 256
NBIS = 18
ARENA_F32 = 51200
ENGS = ["pe", "act", "dve", "pool", "sp"]


class Op:
    __slots__ = ("eng", "fn", "deps", "signal", "dma_sem", "val")


class Prog:
    def __init__(self):
        self.ops = {e: [] for e in ENGS}
        self.lastw = {}
        self.readers = {}
        self.pending_barrier = {e: [] for e in ENGS}
        self.dma_count = {}
        self.dma_last = {}

    def add(self, eng, fn, reads=(), writes=(), dma_sem=None):
        op = Op()
        op.eng = eng
        op.fn = fn
        op.signal = False
        op.dma_sem = dma_sem
        op.val = 0
        deps = {}

        def add_dep(o):
            if o is op:
                return
            if o.eng == "pe" and eng == "pe" and o.dma_sem is None and dma_sem is None:
                return
            deps[id(o)] = o

        for k in reads:
            w = self.lastw.get(k)
            if w is not None:
                add_dep(w)
        for k in writes:
            w = self.lastw.get(k)
            if w is not None:
                add_dep(w)
            rd = self.readers.get(k)
            if rd:
                for r in rd.values():
                    add_dep(r)
        for o in self.pending_barrier[eng]:
            add_dep(o)
        self.pending_barrier[eng] = []
        if dma_sem is not None:
            prev = self.dma_last.get(dma_sem)
            if prev is not None:
                add_dep(prev)
            c = self.dma_count.get(dma_sem, 0) + 1
            self.dma_count[dma_sem] = c
            op.val = 16 * c
            self.dma_last[dma_sem] = op
        for d in deps.values():
            d.signal = True
        op.deps = list(deps.values())
        rid = dma_sem if dma_sem is not None else eng
        for k in writes:
            self.lastw[k] = op
            self.readers[k] = {}
        for k in reads:
            self.readers.setdefault(k, {})[rid] = op
        self.ops[eng].append(op)
        return op

    def barrier(self):
        lasts = []
        for e in ENGS:
            for o in reversed(self.ops[e]):
                if o.dma_sem is None:
                    lasts.append(o)
                    break
        lasts.extend(self.dma_last.values())
        for e in ENGS:
            self.pending_barrier[e] = list(lasts)

    def emit(self, nc, block, stack):
        engsem = {e: stack.enter_context(nc.semaphore("s_" + e)) for e in ENGS}
        dmasem = {k: stack.enter_context(nc.semaphore("d_" + k)) for k in self.dma_count}
        assert len(dmasem) + len(engsem) < 100, len(dmasem)
        for e in ENGS:
            c = 0
            for op in self.ops[e]:
                if op.dma_sem is not None:
                    continue
                if op.signal:
                    c += 1
                    op.val = c
        blk = {"pe": block.tensor, "act": block.scalar, "dve": block.vector,
               "pool": block.gpsimd, "sp": block.sync}
        for e in ENGS:
            ops = self.ops[e]
            if not ops:
                continue

            def body(engh, e=e, ops=ops):
                seen = {}
                for op in ops:
                    waits = {}
                    for d in op.deps:
                        if d.dma_sem is not None:
                            key = "d_" + d.dma_sem
                            s = dmasem[d.dma_sem]
                        else:
                            key = "e_" + d.eng
                            s = engsem[d.eng]
                        if waits.get(key, (None, 0))[1] < d.val:
                            waits[key] = (s, d.val)
                    for key, (s, v) in waits.items():
                        if seen.get(key, 0) >= v:
                            continue
                        engh.wait_ge(s, v)
                        seen[key] = v
                    ins = op.fn(engh)
                    if op.dma_sem is not None:
                        ins.then_inc(dmasem[op.dma_sem], 16)
                    elif op.signal:
                        ins.then_inc(engsem[e], 1)

            blk[e](body)


class Arena:
    def __init__(self, ap, n):
        self.ap = ap
        self.n = n
        self.top = 0
        self.uid = 0

    def alloc(self, free_shape, dtype):
        nel = int(np.prod(free_shape))
        nbytes = nel * (4 if dtype == F32 else 2)
        nfl = ((nbytes + 63) // 64) * 16
        off = self.top
        self.top += nfl
        assert self.top <= self.n, ("arena overflow", self.top, self.n)
        a = self.ap[:, off:off + nfl]
        if dtype != F32:
            a = a.bitcast(dtype)
        a = a[:, :nel]
        if len(free_shape) == 2:
            a = a.rearrange("p (a b) -> p a b", b=free_shape[1])
        elif len(free_shape) == 3:
            a = a.rearrange("p (a b c) -> p a b c", b=free_shape[1], c=free_shape[2])
        self.uid += 1
        return a

    def mark(self):
        return self.top

    def release(self, m):
        self.top = m


class Ring:
    def __init__(self, arena, n, free_shape, dtype, name):
        self.aps = [arena.alloc(free_shape, dtype) for _ in range(n)]
        self.name = name
        self.i = 0

    def next(self):
        k = self.i % len(self.aps)
        self.i += 1
        return self.aps[k], (self.name, k), "%s%d" % (self.name, k)


def tile_w(W):
    K, M = W.shape
    KC, MC = K // 128, M // 128
    return np.ascontiguousarray(W.reshape(KC, 128, MC, 128).transpose(2, 1, 0, 3))


def col_layout(v):
    return np.ascontiguousarray(v.reshape(-1, 128).T)


def rope_tables(dh_rot_half, rows_pattern):
    half = dh_rot_half
    inv = (500000.0 ** (-np.arange(half, dtype=np.float32) / np.float32(half))).astype(np.float32)
    pos = np.arange(TP, dtype=np.float32)
    ang = (pos[:, None] * inv[None, :]).astype(np.float32)
    cos = np.cos(ang).astype(np.float32).T
    sin = np.sin(ang).astype(np.float32).T
    C = np.ones((128, TP), np.float32)
    S = np.zeros((128, TP), np.float32)
    P = np.zeros((128, 128), np.float32)
    for r0 in rows_pattern:
        C[r0:r0 + half] = cos
        C[r0 + half:r0 + 2 * half] = cos
        S[r0:r0 + half] = -sin
        S[r0 + half:r0 + 2 * half] = sin
        for m in range(half):
            P[r0 + half + m, r0 + m] = 1.0
            P[r0 + m, r0 + half + m] = 1.0
    return C, S, P


CONST_COLS = {}
_c = 0
for _name, _w in [("ident", 128), ("ones", 128), ("onesblk", 128), ("PA", 128), ("PI", 128),
                  ("negdiag", 128), ("mdiagT", 128), ("mprevT", 128), ("pow2", 32),
                  ("g_f1_0", 16), ("g_f1_1", 16), ("g_f2_0", 16), ("g_f2_1", 16), ("g_a", 16),
                  ("g_kv", 16), ("g_b", 16), ("g_aq", 1), ("g_ak", 1), ("g_ik", 1), ("g_kvk", 1),
                  ("g_bq", 1), ("eps", 1), ("sinkT", 16)]:
    CONST_COLS[_name] = (_c, _w)
    _c += _w
CW = _c


def build_consts(inp):
    cp = np.zeros((128, CW), np.float32)

    def put(name, arr):
        a, w = CONST_COLS[name]
        cp[:, a:a + w] = arr

    put("ident", np.eye(128, dtype=np.float32))
    put("ones", np.ones((128, 128), np.float32))
    ob = np.zeros((128, 128), np.float32)
    ob[:64, :64] = 1.0
    ob[64:, 64:] = 1.0
    put("onesblk", ob)
    CA, SA, PA = rope_tables(16, [0])
    CI, SI, PI = rope_tables(8, [0, 64])
    put("PA", PA)
    put("PI", PI)
    q = np.arange(128)[:, None]
    s = np.arange(128)[None, :]
    put("negdiag", np.where(s <= q, 0.0, -1e30).astype(np.float32))
    sT = np.arange(128)[:, None]
    qT = np.arange(128)[None, :]
    put("mdiagT", (qT >= sT).astype(np.float32))
    put("mprevT", (qT < sT).astype(np.float32))
    p2 = np.zeros((128, 32), np.float32)
    p2[:, :] = (2.0 ** -(np.arange(32, dtype=np.float32) + 1.0))[None, :]
    put("pow2", p2)
    put("g_f1_0", col_layout(inp["ffn1_norm"][0]))
    put("g_f1_1", col_layout(inp["ffn1_norm"][1]))
    put("g_f2_0", col_layout(inp["ffn2_norm"][0]))
    put("g_f2_1", col_layout(inp["ffn2_norm"][1]))
    put("g_a", col_layout(inp["a_norm"][0]))
    put("g_kv", col_layout(inp["kv_norm"]))
    put("g_b", col_layout(inp["b_norm"][0]))
    put("g_aq", inp["a_q_norm"][0].reshape(128, 1))
    put("g_ak", inp["a_k_norm"][0].reshape(128, 1))
    gik = np.zeros((128, 1), np.float32)
    gik[:64, 0] = inp["a_idx_k_norm"][0]
    put("g_ik", gik)
    put("g_kvk", np.concatenate([inp["kv_k_norm"], inp["kv_k_norm"]]).reshape(128, 1))
    put("g_bq", np.concatenate([inp["b_q_norm"][0], inp["b_q_norm"][0]]).reshape(128, 1))
    put("eps", np.full((128, 1), EPS, np.float32))
    sk = inp["b_sinks"][0].reshape(16, 2)
    st = np.zeros((128, 16), np.float32)
    st[:64, :] = sk[:, 0][None, :]
    st[64:, :] = sk[:, 1][None, :]
    put("sinkT", st)
    tabs = np.ascontiguousarray(np.stack([CA, SA, CI, SI], axis=1))
    return cp, tabs


def prep_weights(inp):
    w = {}
    for L in range(2):
        for which in (1, 2):
            pre = "ffn%d" % which
            w["f%dg_%d" % (which, L)] = tile_w(inp[pre + "_w_gate"][L])
            w["f%du_%d" % (which, L)] = tile_w(inp[pre + "_w_up"][L])
            wd = inp[pre + "_w_down"][L]
            f0 = 0
            for g, kc in enumerate(FGROUPS):
                w["f%dd_%d_g%d" % (which, L, g)] = tile_w(wd[f0 * 128:(f0 + kc) * 128, :])
                f0 += kc
    win = np.zeros((D, A_IN_PAD), np.float32)
    win[:, :A_IN] = inp["a_w_in"][0]
    w["a_in"] = tile_w(win)
    w["a_out"] = tile_w(inp["a_w_out"][0])
    w["kv"] = tile_w(inp["kv_w"])
    w["b_q"] = tile_w(inp["b_w_q"][0])
    w["b_out"] = tile_w(inp["b_w_out"][0])
    return w


WSHAPES = {}
for _L in range(2):
    for _wh in (1, 2):
        WSHAPES["f%dg_%d" % (_wh, _L)] = (KF, 128, KD, 128)
        WSHAPES["f%du_%d" % (_wh, _L)] = (KF, 128, KD, 128)
        for _g, _kc in enumerate(FGROUPS):
            WSHAPES["f%dd_%d_g%d" % (_wh, _L, _g)] = (KD, 128, _kc, 128)
WSHAPES["a_in"] = (33, 128, KD, 128)
WSHAPES["a_out"] = (KD, 128, KD, 128)
WSHAPES["kv"] = (4, 128, KD, 128)
WSHAPES["b_q"] = (KD, 128, KD, 128)
WSHAPES["b_out"] = (KD, 128, KD, 128)
A_IN_ORDER = [32] + list(range(32))


ALL_PHASES = ["f01", "ain", "acore", "aout", "f02", "kv", "f11", "bq", "bcore", "bout", "f12"]


def build_nc(phases=None, debug=False):
    phases = list(ALL_PHASES) if phases is None else list(phases)
    nc = bass.Bass("TRN2", target_bir_lowering=False)
    P = Prog()
    stack = ExitStack()

    x_d = nc.dram_tensor("x", [SEQ, D], F32, kind="ExternalInput").ap()
    meta_d = nc.dram_tensor("meta", [NMETA, D], F32, kind="ExternalInput").ap()
    cp_d = nc.dram_tensor("cpack", [128, CW], F32, kind="ExternalInput").ap()
    tabs_d = nc.dram_tensor("tabs", [128, 4, TP], F32, kind="ExternalInput").ap()
    wd = {k: nc.dram_tensor("w_" + k, list(s), F32, kind="ExternalInput").ap() for k, s in WSHAPES.items()}
    out_d = nc.dram_tensor("out", [SEQ, D], F32, kind="ExternalOutput").ap()
    hT_d = nc.dram_tensor("hT", [D, T], F32, kind="Internal").ap()
    qA_d = nc.dram_tensor("qA", [D, TP], BF16, kind="Internal").ap()
    oT_d = nc.dram_tensor("oT", [D, TP], BF16, kind="Internal").ap()
    kA_d = nc.dram_tensor("kA", [512, TP], BF16, kind="Internal").ap()
    vA_d = nc.dram_tensor("vA", [512, TP], BF16, kind="Internal").ap()
    qi_d = nc.dram_tensor("qi", [1024, TP], BF16, kind="Internal").ap()
    ki_d = nc.dram_tensor("ki", [128, TP], BF16, kind="Internal").ap()
    ksh_d = nc.dram_tensor("ksh", [256, TP], BF16, kind="Internal").ap()
    vsh_d = nc.dram_tensor("vsh", [256, TP], BF16, kind="Internal").ap()
    dbg_d = None
    if debug:
        dbg_d = nc.dram_tensor("dbg", [D, T], F32, kind="ExternalOutput").ap()
    hT_v = hT_d.rearrange("(kc p) t -> p kc t", p=128)

    arena_t = stack.enter_context(nc.sbuf_tensor("arena", [128, ARENA_F32], F32))
    AR = Arena(arena_t[:, :], ARENA_F32)
    psb = [stack.enter_context(nc.psum_tensor("ps%d" % i, [128, 512], F32)) for i in range(8)]

    def PS(i):
        return psb[i][:, :]

    def PK(i):
        return ("ps", i)

    cpk = AR.alloc((CW,), F32)

    def CC(name, a=0, b=None):
        c0, w = CONST_COLS[name]
        if b is None:
            b = w
        return cpk[:, c0 + a:c0 + b]

    ident_bf = AR.alloc((128,), BF16)
    ones_bf = AR.alloc((128,), BF16)
    onesblk_bf = AR.alloc((128,), BF16)
    mdiagT_bf = AR.alloc((128,), BF16)
    mprevT_bf = AR.alloc((128,), BF16)
    NS, NB = 3, 4
    wst = [AR.alloc((KD, 128), F32) for _ in range(NS)]
    wbf = [AR.alloc((KD, 128), BF16) for _ in range(NB)]

    plan = []

    def plan_ffn(L, which):
        f0 = 0
        for g, kcg in enumerate(FGROUPS):
            for ml in range(kcg):
                plan.append(("f%dg_%d" % (which, L), f0 + ml, KD))
                plan.append(("f%du_%d" % (which, L), f0 + ml, KD))
            for dc in range(KD):
                plan.append(("f%dd_%d_g%d" % (which, L, g), dc, kcg))
            f0 += kcg

    def plan_all():
        for ph in ALL_PHASES:
            if ph not in phases:
                continue
            if ph == "f01":
                plan_ffn(0, 1)
            elif ph == "ain":
                for m in A_IN_ORDER:
                    plan.append(("a_in", m, KD))
            elif ph == "aout":
                for m in range(KD):
                    plan.append(("a_out", m, KD))
            elif ph == "f02":
                plan_ffn(0, 2)
            elif ph == "kv":
                for m in range(4):
                    plan.append(("kv", m, KD))
            elif ph == "f11":
                plan_ffn(1, 1)
            elif ph == "bq":
                for m in range(KD):
                    plan.append(("b_q", m, KD))
            elif ph == "bout":
                for m in range(KD):
                    plan.append(("b_out", m, KD))
            elif ph == "f12":
                plan_ffn(1, 2)

    plan_all()

    class WS:
        rec = 0
        used = 0
        PF = 3

    def ws_record_upto(n):
        n = min(n, len(plan))
        while WS.rec < n:
            j = WS.rec
            name, m, kc = plan[j]
            ss_, bs_ = j % NS, j % NB
            src = wd[name][m]
            P.add("sp", lambda e, o=wst[ss_][:, :kc, :], i=src: e.dma_start(out=o, in_=i),
                  writes=[("wst", ss_)], dma_sem="wst%d" % ss_)
            P.add("pool", lambda e, o=wbf[bs_][:, :kc, :], i=wst[ss_][:, :kc, :]: e.tensor_copy(out=o, in_=i),
                  reads=[("wst", ss_)], writes=[("wbf", bs_)])
            WS.rec += 1

    def ws_get(name, m):
        j = WS.used
        assert plan[j][0] == name and plan[j][1] == m, (plan[j], name, m)
        assert j < WS.rec
        WS.used += 1
        return wbf[j % NB], ("wbf", j % NB)

    def ws_advance():
        ws_record_upto(WS.used + WS.PF)

    P.add("act", lambda e: e.dma_start(out=cpk, in_=cp_d), writes=["cpk"], dma_sem="cst")
    for dst, nm in [(ident_bf, "ident"), (ones_bf, "ones"), (onesblk_bf, "onesblk"),
                    (mdiagT_bf, "mdiagT"), (mprevT_bf, "mprevT")]:
        P.add("dve", lambda e, o=dst, i=CC(nm): e.tensor_copy(out=o, in_=i), reads=["cpk"], writes=["cbf"])
    CK = ["cpk", "cbf"]
    ws_record_upto(WS.PF)

    m_init = AR.mark()
    zt = AR.alloc((8, TP - T), BF16)
    P.add("pool", lambda e: e.memset(zt, 0.0), writes=["zt"])
    zi = 0
    for dt_, rows in [(qA_d, D), (oT_d, D), (kA_d, 512), (vA_d, 512), (qi_d, 1024), (ki_d, 128),
                      (ksh_d, 256), (vsh_d, 256)]:
        nchunk = rows // 128
        for c0 in range(0, nchunk, 8):
            n = min(8, nchunk - c0)
            dst = dt_.rearrange("(c p) t -> p c t", p=128)[:, c0:c0 + n, T:TP]
            P.add("act", lambda e, o=dst, i=zt[:, :n, :]: e.dma_start(out=o, in_=i),
                  reads=["zt"], writes=[("pad", zi)], dma_sem="zero%d" % (zi % 2))
            zi += 1

    xs = Ring(AR, 2, (D,), F32, "xs")
    xT = Ring(AR, 2, (KD, 128), F32, "xT")
    for i in range(NT):
        ntok = 128 if i < 16 else NMETA
        src = x_d[128 * i:128 * (i + 1), :] if i < 16 else meta_d[:, :]
        tok0 = NMETA + 128 * i if i < 16 else 0
        xa, xk, xsem = xs.next()
        P.add("act", lambda e, o=xa[:ntok, :], s=src: e.dma_start(out=o, in_=s), writes=[xk], dma_sem=xsem)
        ta, tk, tsem = xT.next()
        for q4 in range(4):
            bank = q4 % 2
            for r in range(4):
                kc = q4 * 4 + r
                P.add("pe", lambda e, o=PS(bank)[:, r * 128:r * 128 + ntok], a=xa[:ntok, kc * 128:(kc + 1) * 128],
                      idn=CC("ident")[:ntok, :ntok]: e.transpose(out=o, in_=a, identity=idn),
                      reads=[xk] + CK, writes=[PK(bank)])
            srcv = PS(bank).rearrange("p (a b) -> p a b", b=128)[:, :, :ntok]
            P.add("dve" if q4 % 2 else "act",
                  (lambda e, o=ta[:, q4 * 4:q4 * 4 + 4, :ntok], s=srcv: e.tensor_copy(out=o, in_=s)) if q4 % 2 else
                  (lambda e, o=ta[:, q4 * 4:q4 * 4 + 4, :ntok], s=srcv: e.activation(out=o, in_=s, func=AF.Copy)),
                  reads=[PK(bank)], writes=[tk])
        P.add("act", lambda e, o=hT_v[:, :, tok0:tok0 + ntok], s=ta[:, :, :ntok]: e.dma_start(out=o, in_=s),
              reads=[tk], writes=[("hTinit", i)], dma_sem=tsem)
    P.barrier()
    AR.release(m_init)

    def chunk(c):
        return slice(c * TC, (c + 1) * TC)

    def norm_to_nT(gname, nT, hs, sqs, rsr):
        for c in range(NCH):
            bank = 6 + (c % 2)
            tiles = []
            for kc in range(KD):
                ha, hk, hsem = hs.next()
                P.add("act", lambda e, o=ha, s=hT_v[:, kc, chunk(c)]: e.dma_start(out=o, in_=s),
                      reads=[("hT", kc, c)], writes=[hk], dma_sem=hsem)
                tiles.append((ha, hk))
                if kc >= 2:
                    _sq(tiles[kc - 2], kc - 2, bank, sqs)
            _sq(tiles[KD - 2], KD - 2, bank, sqs)
            _sq(tiles[KD - 1], KD - 1, bank, sqs)
            ra, rk, _ = rsr.next()
            P.add("act", lambda e, o=ra, i=PS(bank)[:, :TC]: e.activation(out=o, in_=i, func=AF.Sqrt, bias=CC("eps"), scale=1.0 / D),
                  reads=[PK(bank)] + CK, writes=[rk])
            P.add("dve", lambda e, o=ra: e.reciprocal(out=o, in_=o), reads=[rk], writes=[rk])
            for kc in range(KD):
                ha, hk, hsem = hs.next()
                P.add("act", lambda e, o=ha, s=hT_v[:, kc, chunk(c)]: e.dma_start(out=o, in_=s),
                      reads=[("hT", kc, c)], writes=[hk], dma_sem=hsem)
                P.add("dve", lambda e, o=nT[:, kc, chunk(c)], a=ha, g=CC(gname, kc, kc + 1), r=ra:
                      e.scalar_tensor_tensor(out=o, in0=a, scalar=g, in1=r, op0=ALU.mult, op1=ALU.mult),
                      reads=[hk, rk] + CK, writes=[("nT", kc, c)])

    def _sq(tile, kc, bank, sqs):
        ha, hk = tile
        sa, sk, _ = sqs.next()
        P.add("act", lambda e, o=sa, i=ha: e.activation(out=o, in_=i, func=AF.Square), reads=[hk], writes=[sk])
        P.add("pe", lambda e, o=PS(bank)[:, :TC], r=sa: e.matmul(o, ones_bf, r, start=(kc == 0), stop=(kc == KD - 1)),
              reads=[sk] + CK, writes=[PK(bank)])

    def residual_epilogue(bank, dc, c, hs, sts, scale):
        ha, hk, hsem = hs.next()
        P.add("act", lambda e, o=ha, s=hT_v[:, dc, chunk(c)]: e.dma_start(out=o, in_=s),
              reads=[("hT", dc, c)], writes=[hk], dma_sem=hsem)
        sa, sk, ssem = sts.next()
        P.add("dve", lambda e, o=sa, y=PS(bank)[:, :TC], h=ha:
              e.scalar_tensor_tensor(out=o, in0=y, scalar=scale, in1=h, op0=ALU.mult, op1=ALU.add),
              reads=[PK(bank), hk], writes=[sk])
        P.add("act", lambda e, o=hT_v[:, dc, chunk(c)], s=sa: e.dma_start(out=o, in_=s),
              reads=[sk], writes=[("hT", dc, c)], dma_sem=ssem)

    def gemm(wname, morder, kc_n, rhs, rhs_key, epilogue, banks):
        bi = 0
        for m in morder:
            wt, wk = ws_get(wname, m)
            for c in range(NCH):
                bank = banks[bi % len(banks)]
                bi += 1
                for kc in range(kc_n):
                    P.add("pe", lambda e, o=PS(bank)[:, :TC], l=wt[:, kc, :], r=rhs(kc, c), kc=kc:
                          e.matmul(o, l, r, start=(kc == 0), stop=(kc == kc_n - 1)),
                          reads=[wk, rhs_key(kc, c)], writes=[PK(bank)])
                epilogue(m, c, bank)
            ws_advance()

    def ffn(L, which):
        m0 = AR.mark()
        nT = AR.alloc((KD, T), BF16)
        aT = AR.alloc((max(FGROUPS), T), BF16)
        hs = Ring(AR, 6, (TC,), F32, "hs")
        sqs = Ring(AR, 3, (TC,), BF16, "sq")
        rsr = Ring(AR, 2, (TC,), F32, "rs")
        sgs = Ring(AR, 3, (TC,), F32, "sg")
        sts = Ring(AR, 3, (TC,), F32, "st")
        norm_to_nT("g_f%d_%d" % (which, L), nT, hs, sqs, rsr)
        gname, uname = "f%dg_%d" % (which, L), "f%du_%d" % (which, L)
        f0 = 0
        gi = 0
        for g, kcg in enumerate(FGROUPS):
            for ml in range(kcg):
                wg, wgk = ws_get(gname, f0 + ml)
                wu, wuk = ws_get(uname, f0 + ml)
                for c in range(NCH):
                    bg = gi % 2
                    bu = 2 + gi % 2
                    gi += 1
                    for kc in range(KD):
                        P.add("pe", lambda e, o=PS(bg)[:, :TC], l=wg[:, kc, :], r=nT[:, kc, chunk(c)], kc=kc:
                              e.matmul(o, l, r, start=(kc == 0), stop=(kc == KD - 1)),
                              reads=[wgk, ("nT", kc, c)], writes=[PK(bg)])
                    for kc in range(KD):
                        P.add("pe", lambda e, o=PS(bu)[:, :TC], l=wu[:, kc, :], r=nT[:, kc, chunk(c)], kc=kc:
                              e.matmul(o, l, r, start=(kc == 0), stop=(kc == KD - 1)),
                              reads=[wuk, ("nT", kc, c)], writes=[PK(bu)])
                    sa, sk, _ = sgs.next()
                    P.add("act", lambda e, o=sa, i=PS(bg)[:, :TC]: e.activation(out=o, in_=i, func=AF.Silu),
                          reads=[PK(bg)], writes=[sk])
                    P.add("dve", lambda e, o=aT[:, ml, chunk(c)], u=PS(bu)[:, :TC], s=sa:
                          e.tensor_tensor(out=o, in0=u, in1=s, op=ALU.mult),
                          reads=[PK(bu), sk], writes=[("aT", ml, c)])
                ws_advance()
            gemm("f%dd_%d_g%d" % (which, L, g), range(KD), kcg,
                 lambda kc, c: aT[:, kc, chunk(c)], lambda kc, c: ("aT", kc, c),
                 lambda m, c, bank: residual_epilogue(bank, m, c, hs, sts, 0.5), [4, 5])
            f0 += kcg
        P.barrier()
        AR.release(m0)

    class HE:
        pass

    def head_epilogue_setup(tabs_idx):
        he = HE()
        he.tab = AR.alloc((2, TP), F32)
        P.add("act", lambda e: e.dma_start(out=he.tab, in_=tabs_d[:, tabs_idx:tabs_idx + 2, :]),
              writes=[("tab", tabs_idx)], dma_sem="tab%d" % (tabs_idx // 2))
        he.tk = ("tab", tabs_idx)
        return he

    def head_epilogue(bank, c, R, he, ones_ap, inv_dh, gain, Pm, dst_ap, dst_key, ssbank, rpbank, extra=None):
        qa, qk, _ = R["qf"].next()
        P.add("act", lambda e, o=qa, i=PS(bank)[:, :TC]: e.activation(out=o, in_=i, func=AF.Copy), reads=[PK(bank)], writes=[qk])
        if extra is not None:
            extra(bank, c)
        if ones_ap is not None:
            sa, sk, _ = R["sq"].next()
            P.add("act", lambda e, o=sa, i=PS(bank)[:, :TC]: e.activation(out=o, in_=i, func=AF.Square), reads=[PK(bank)], writes=[sk])
            P.add("pe", lambda e, o=PS(ssbank)[:, :TC], l=ones_ap, r=sa: e.matmul(o, l, r, start=True, stop=True),
                  reads=[sk] + CK, writes=[PK(ssbank)])
            ra, rk, _ = R["rs"].next()
            P.add("act", lambda e, o=ra, i=PS(ssbank)[:, :TC]: e.activation(out=o, in_=i, func=AF.Sqrt, bias=CC("eps"), scale=inv_dh),
                  reads=[PK(ssbank)] + CK, writes=[rk])
            P.add("dve", lambda e, o=ra: e.reciprocal(out=o, in_=o), reads=[rk], writes=[rk])
            P.add("dve", lambda e, o=qa, g=gain, r=ra: e.scalar_tensor_tensor(out=o, in0=o, scalar=g, in1=r, op0=ALU.mult, op1=ALU.mult),
                  reads=[qk, rk] + CK, writes=[qk])
        P.add("pe", lambda e, o=PS(rpbank)[:, :TC], l=Pm, r=qa: e.matmul(o, l, r, start=True, stop=True),
              reads=[qk] + CK, writes=[PK(rpbank)])
        ta, tk, _ = R["t1"].next()
        P.add("dve", lambda e, o=ta, a=PS(rpbank)[:, :TC], s=he.tab[:, 1, chunk(c)]: e.tensor_tensor(out=o, in0=a, in1=s, op=ALU.mult),
              reads=[PK(rpbank), he.tk], writes=[tk])
        P.add("pool", lambda e, o=qa, s=he.tab[:, 0, chunk(c)]: e.tensor_tensor(out=o, in0=o, in1=s, op=ALU.mult),
              reads=[qk, he.tk], writes=[qk])
        oa, ok, osem = R["ob"].next()
        P.add("pool", lambda e, o=oa, a=qa, b=ta: e.tensor_tensor(out=o, in0=a, in1=b, op=ALU.add),
              reads=[qk, tk], writes=[ok])
        P.add("act", lambda e, o=dst_ap, s=oa: e.dma_start(out=o, in_=s), reads=[ok], writes=[dst_key], dma_sem=osem)

    def plain_epilogue(bank, c, R, dst_ap, dst_key):
        oa, ok, osem = R["ob"].next()
        P.add("act", lambda e, o=oa, i=PS(bank)[:, :TC]: e.activation(out=o, in_=i, func=AF.Copy), reads=[PK(bank)], writes=[ok])
        P.add("act", lambda e, o=dst_ap, s=oa: e.dma_start(out=o, in_=s), reads=[ok], writes=[dst_key], dma_sem=osem)

    def he_rings():
        return {"qf": Ring(AR, 3, (TC,), F32, "qf"), "sq": Ring(AR, 2, (TC,), BF16, "hsq"),
                "rs": Ring(AR, 2, (TC,), F32, "hrs"), "t1": Ring(AR, 2, (TC,), F32, "t1"),
                "ob": Ring(AR, 3, (TC,), BF16, "ob")}

    def attnA_inproj(wiT):
        m0 = AR.mark()
        nT = AR.alloc((KD, T), BF16)
        hs = Ring(AR, 6, (TC,), F32, "hs")
        sqs = Ring(AR, 3, (TC,), BF16, "sq")
        rsr = Ring(AR, 2, (TC,), F32, "rs")
        norm_to_nT("g_a", nT, hs, sqs, rsr)
        heA = head_epilogue_setup(0)
        heI = head_epilogue_setup(2)
        R = he_rings()
        cnt = {"i": 0}

        def epi(m, c, bank):
            k = cnt["i"]
            cnt["i"] += 1
            ssb, rpb = 4 + k % 2, 6 + k % 2
            tok = chunk(c)
            if m < 16:
                head_epilogue(bank, c, R, heA, ones_bf, 1.0 / 128, CC("g_aq"), CC("PA"),
                              qA_d[m * 128:(m + 1) * 128, tok], ("qA", m, c), ssb, rpb)
            elif m < 20:
                head_epilogue(bank, c, R, heA, ones_bf, 1.0 / 128, CC("g_ak"), CC("PA"),
                              kA_d[(m - 16) * 128:(m - 15) * 128, tok], ("kA", m, c), ssb, rpb)
            elif m < 24:
                plain_epilogue(bank, c, R, vA_d[(m - 20) * 128:(m - 19) * 128, tok], ("vA", m, c))
            elif m < 32:
                head_epilogue(bank, c, R, heI, None, 0.0, None, CC("PI"),
                              qi_d[(m - 24) * 128:(m - 23) * 128, tok], ("qi", m, c), ssb, rpb)
            else:
                def extra(bank, c):
                    P.add("act", lambda e, o=wiT[64:80, chunk(c)], i=PS(bank)[64:80, :TC]:
                          e.activation(out=o, in_=i, func=AF.Copy, scale=float(1024 ** -0.5)),
                          reads=[PK(bank)], writes=[("wiT", c)])
                head_epilogue(bank, c, R, heI, onesblk_bf, 1.0 / 64, CC("g_ik"), CC("PI"),
                              ki_d[:, tok], ("ki", c), ssb, rpb, extra=extra)

        gemm("a_in", A_IN_ORDER, KD, lambda kc, c: nT[:, kc, chunk(c)], lambda kc, c: ("nT", kc, c), epi, [0, 1, 2, 3])
        P.barrier()
        AR.release(m0)

    def attnA_core(wiT):
        m0 = AR.mark()
        kT = AR.alloc((4, TP), BF16)
        vtok = AR.alloc((NT, 512), BF16)
        qiT = AR.alloc((8, TP), BF16)
        kidup = AR.alloc((TP,), BF16)
        wtok = AR.alloc((NT, 16), F32)
        m_vt = AR.mark()
        vT = AR.alloc((4, TP), BF16)
        P.add("act", lambda e: e.dma_start(out=vT, in_=vA_d.rearrange("(c p) t -> p c t", p=128)), reads=[("vA",)], writes=["vT"], dma_sem="bigB")
        for j in range(NT):
            for n in range(4):
                P.add("pe", lambda e, o=PS(2).bitcast(BF16)[:, n * 128:(n + 1) * 128], a=vT[:, n, 128 * j:128 * (j + 1)]:
                      e.transpose(out=o, in_=a, identity=ident_bf), reads=["vT"] + CK, writes=[PK(2)])
            P.add("act", lambda e, o=vtok[:, j, :], i=PS(2).bitcast(BF16)[:, 0:512]: e.activation(out=o, in_=i, func=AF.Copy),
                  reads=[PK(2)], writes=[("vtok", j)])
        P.barrier()
        AR.release(m_vt)
        accs = [AR.alloc((TP,), F32) for _ in range(2)]
        junk = AR.alloc((TP,), BF16)
        maskq = AR.alloc((TP,), BF16)
        maskTs = [AR.alloc((NT, 128), BF16) for _ in range(2)]
        rls = Ring(AR, 3, (512,), F32, "rl")
        es = Ring(AR, 3, (512,), BF16, "es")
        pms = Ring(AR, 3, (512,), BF16, "pm")
        qts = Ring(AR, 2, (16, 128), BF16, "qt")
        osts = Ring(AR, 2, (16, 128), BF16, "ost")
        rdens = Ring(AR, 2, (512,), F32, "rden")
        bv = AR.alloc((8 + NBIS + 1,), F32)
        mx, mn, rr, cand, cnt, sg, thr = [bv[:, i:i + 1] for i in range(7)]
        steps = bv[:, 8:8 + NBIS + 1]

        def kv(a):
            return a.rearrange("(c p) t -> p c t", p=128)
        P.add("act", lambda e: e.dma_start(out=kT, in_=kv(kA_d)), reads=[("kA",)], writes=["kT"], dma_sem="bigA")
        P.add("act", lambda e: e.dma_start(out=qiT, in_=kv(qi_d)), reads=[("qi",)], writes=["qiT"], dma_sem="bigC")
        P.add("act", lambda e: e.dma_start(out=kidup[0:64, :], in_=ki_d[0:64, :]), writes=["kidup0"], dma_sem="bigD")
        P.add("act", lambda e: e.dma_start(out=kidup[64:128, :], in_=ki_d[0:64, :]), writes=["kidup1"], dma_sem="bigE")
        KI = ["kidup0", "kidup1"]
        for i in range(NT):
            P.add("pe", lambda e, o=PS(2)[:, 0:16], l=wiT[64:80, 128 * i:128 * (i + 1)], r=CC("ident")[64:80, 64:80]:
                  e.matmul(o, l, r, start=True, stop=True), reads=["wiT_all"] + CK, writes=[PK(2)])
            P.add("dve", lambda e, o=wtok[:, i, :], s=PS(2)[:, 0:16]: e.tensor_copy(out=o, in_=s), reads=[PK(2)], writes=[("wtok", i)])

        st = {"sb": 0}

        def scores(i):
            N = 128 * (i + 1)
            acc = accs[i % 2]
            ak = ("acc", i % 2)
            for b0 in range(0, N, 512):
                w = min(512, N - b0)
                for h in range(16):
                    c, base = h // 2, (h % 2) * 64
                    bank = st["sb"] % 2
                    st["sb"] += 1
                    P.add("pe", lambda e, o=PS(bank)[:, :w], l=qiT[base:base + 64, c, 128 * i:128 * (i + 1)],
                          r=kidup[base:base + 64, b0:b0 + w]: e.matmul(o, l, r, start=True, stop=True),
                          reads=["qiT", KI[h % 2]], writes=[PK(bank)])
                    ra, rk, _ = rls.next()
                    P.add("act", lambda e, o=ra[:, :w], s=PS(bank)[:, :w]: e.activation(out=o, in_=s, func=AF.Relu),
                          reads=[PK(bank)], writes=[rk])
                    if h == 0:
                        P.add("dve", lambda e, o=acc[:, b0:b0 + w], a=ra[:, :w], s=wtok[:, i, 0:1]:
                              e.tensor_scalar(out=o, in0=a, scalar1=s, scalar2=None, op0=ALU.mult),
                              reads=[rk, ("wtok", i)], writes=[ak])
                    else:
                        P.add("dve", lambda e, o=acc[:, b0:b0 + w], a=ra[:, :w], s=wtok[:, i, h:h + 1]:
                              e.scalar_tensor_tensor(out=o, in0=a, scalar=s, in1=o, op0=ALU.mult, op1=ALU.add),
                              reads=[rk, ("wtok", i), ak], writes=[ak])
            P.add("dve", lambda e, o=acc[:, 128 * i:N]: e.tensor_tensor(out=o, in0=o, in1=CC("negdiag"), op=ALU.add),
                  reads=[ak] + CK, writes=[ak])

        def select(i):
            N = 128 * (i + 1)
            acc = accs[i % 2]
            ak = ("acc", i % 2)
            BK = "bis"
            if i < 2:
                P.add("dve", lambda e: e.memset(thr, -1e29), writes=[BK])
            else:
                P.add("dve", lambda e: e.tensor_reduce(out=mx, in_=acc[:, :N], axis=AX.X, op=ALU.max), reads=[ak], writes=[BK])
                P.add("dve", lambda e: e.tensor_reduce(out=mn, in_=acc[:, :128 * i], axis=AX.X, op=ALU.min), reads=[ak, BK], writes=[BK])
                P.add("dve", lambda e: e.tensor_tensor(out=rr, in0=mx, in1=mn, op=ALU.subtract), reads=[BK], writes=[BK])
                P.add("dve", lambda e: e.tensor_scalar(out=steps, in0=CC("pow2", 0, NBIS + 1), scalar1=rr, scalar2=None, op0=ALU.mult),
                      reads=[BK] + CK, writes=[BK])
                P.add("dve", lambda e: e.tensor_tensor(out=cand, in0=mn, in1=steps[:, 0:1], op=ALU.add), reads=[BK], writes=[BK])
                for k in range(NBIS):
                    P.add("dve", lambda e: e.tensor_scalar(out=junk[:, :N], in0=acc[:, :N], scalar1=cand, scalar2=None,
                                                           op0=ALU.is_ge, op1=ALU.add, accum_out=cnt),
                          reads=[ak, BK], writes=[BK, "junk"])
                    P.add("dve", lambda e: e.tensor_scalar(out=sg, in0=cnt, scalar1=float(TOPK), scalar2=0.5,
                                                           op0=ALU.is_ge, op1=ALU.subtract), reads=[BK], writes=[BK])
                    P.add("dve", lambda e, k=k: e.scalar_tensor_tensor(out=cand, in0=sg, scalar=steps[:, k:k + 1], in1=cand,
                                                                       op0=ALU.mult, op1=ALU.add), reads=[BK], writes=[BK])
                P.add("dve", lambda e: e.tensor_tensor(out=thr, in0=cand, in1=steps[:, NBIS:NBIS + 1], op=ALU.subtract),
                      reads=[BK], writes=[BK])
            P.add("dve", lambda e: e.tensor_scalar(out=maskq[:, :N], in0=acc[:, :N], scalar1=thr, scalar2=None, op0=ALU.is_ge),
                  reads=[ak, BK], writes=["maskq"])
            mT = maskTs[i % 2]
            mk = ("maskT", i % 2)
            for j0 in range(0, i + 1, 8):
                n = min(8, i + 1 - j0)
                for jj in range(n):
                    j = j0 + jj
                    P.add("pe", lambda e, o=PS(2).bitcast(BF16)[:, jj * 128:(jj + 1) * 128], a=maskq[:, 128 * j:128 * (j + 1)]:
                          e.transpose(out=o, in_=a, identity=ident_bf), reads=["maskq"] + CK, writes=[PK(2)])
                P.add("act", lambda e, o=mT[:, j0:j0 + n, :], s=PS(2).bitcast(BF16)[:, :n * 128].rearrange("p (a b) -> p a b", b=128):
                      e.activation(out=o, in_=s, func=AF.Copy), reads=[PK(2)], writes=[mk])

        def attend(i):
            mT = maskTs[i % 2]
            mk = ("maskT", i % 2)
            qa, qk, qsem = qts.next()
            P.add("act", lambda e, o=qa, s=qA_d.rearrange("(h p) t -> p h t", p=128)[:, :, 128 * i:128 * (i + 1)]:
                  e.dma_start(out=o, in_=s), reads=[("qA",)], writes=[qk], dma_sem=qsem)
            oa, ok, osem = osts.next()
            for n in range(4):
                for j in range(i + 1):
                    sb = 3 + (st["sb"] % 2)
                    st["sb"] += 1
                    P.add("pe", lambda e, o=PS(sb).rearrange("p (a b) -> p a b", b=128), l=kT[:, n, 128 * j:128 * (j + 1)], r=qa[:, 4 * n:4 * n + 4, :]:
                          e.matmul(o, l, r, start=True, stop=True), reads=["kT", qk], writes=[PK(sb)])
                    ea, ek, _ = es.next()
                    P.add("act", lambda e, o=ea, s=PS(sb): e.activation(out=o, in_=s, func=AF.Exp, scale=float(128 ** -0.5)),
                          reads=[PK(sb)], writes=[ek])
                    pa, pk, _ = pms.next()
                    P.add("pool", lambda e, o=pa.rearrange("p (a b) -> p a b", b=128), a=ea.rearrange("p (a b) -> p a b", b=128),
                          m=mT[:, j:j + 1, :].to_broadcast([128, 4, 128]): e.tensor_tensor(out=o, in0=a, in1=m, op=ALU.mult),
                          reads=[ek, mk], writes=[pk])
                    P.add("pe", lambda e, l=vtok[:, j, n * 128:(n + 1) * 128], r=pa, j=j: e.matmul(PS(5), l, r, start=(j == 0), stop=(j == i)),
                          reads=[("vtok", j), pk], writes=[PK(5)])
                    P.add("pe", lambda e, r=pa, j=j: e.matmul(PS(6), ones_bf, r, start=(j == 0), stop=(j == i)),
                          reads=[pk] + CK, writes=[PK(6)])
                ra, rk, _ = rdens.next()
                P.add("dve", lambda e, o=ra: e.reciprocal(out=o, in_=PS(6)), reads=[PK(6)], writes=[rk])
                P.add("dve", lambda e, o=oa[:, 4 * n:4 * n + 4, :], r=ra.rearrange("p (a b) -> p a b", b=128):
                      e.tensor_tensor(out=o, in0=PS(5).rearrange("p (a b) -> p a b", b=128), in1=r, op=ALU.mult),
                      reads=[PK(5), rk], writes=[ok])
            P.add("act", lambda e, s=oa, o=oT_d.rearrange("(h p) t -> p h t", p=128)[:, :, 128 * i:128 * (i + 1)]:
                  e.dma_start(out=o, in_=s), reads=[ok], writes=[("oT", i)], dma_sem=osem)

        for i in range(NT):
            scores(i)
            select(i)
            attend(i)
        P.barrier()
        AR.release(m0)

    def outproj(wname, src_d):
        m0 = AR.mark()
        oT = AR.alloc((KD, T), BF16)
        hs = Ring(AR, 6, (TC,), F32, "hs")
        sts = Ring(AR, 3, (TC,), F32, "st")
        for kc in range(KD):
            P.add("act", lambda e, o=oT[:, kc, :], s=src_d[kc * 128:(kc + 1) * 128, 0:T]: e.dma_start(out=o, in_=s),
                  writes=[("oTs", kc)], dma_sem="big%s" % "ABCDE"[kc % 5])
        gemm(wname, range(KD), KD, lambda kc, c: oT[:, kc, chunk(c)], lambda kc, c: ("oTs", kc),
             lambda m, c, bank: residual_epilogue(bank, m, c, hs, sts, 1.0), [0, 1, 2, 3])
        P.barrier()
        AR.release(m0)

    def kv_proj():
        m0 = AR.mark()
        nT = AR.alloc((KD, T), BF16)
        hs = Ring(AR, 6, (TC,), F32, "hs")
        sqs = Ring(AR, 3, (TC,), BF16, "sq")
        rsr = Ring(AR, 2, (TC,), F32, "rs")
        norm_to_nT("g_kv", nT, hs, sqs, rsr)
        heI = head_epilogue_setup(2)
        R = he_rings()
        cnt = {"i": 0}

        def epi(m, c, bank):
            k = cnt["i"]
            cnt["i"] += 1
            tok = chunk(c)
            if m < 2:
                head_epilogue(bank, c, R, heI, onesblk_bf, 1.0 / 64, CC("g_kvk"), CC("PI"),
                              ksh_d[m * 128:(m + 1) * 128, tok], ("ksh", m, c), 4 + k % 2, 6 + k % 2)
            else:
                plain_epilogue(bank, c, R, vsh_d[(m - 2) * 128:(m - 1) * 128, tok], ("vsh", m, c))

        gemm("kv", range(4), KD, lambda kc, c: nT[:, kc, chunk(c)], lambda kc, c: ("nT", kc, c), epi, [0, 1, 2, 3])
        P.barrier()
        AR.release(m0)

    def attnB_qproj():
        m0 = AR.mark()
        nT = AR.alloc((KD, T), BF16)
        hs = Ring(AR, 6, (TC,), F32, "hs")
        sqs = Ring(AR, 3, (TC,), BF16, "sq")
        rsr = Ring(AR, 2, (TC,), F32, "rs")
        norm_to_nT("g_b", nT, hs, sqs, rsr)
        heI = head_epilogue_setup(2)
        R = he_rings()
        cnt = {"i": 0}

        def epi(m, c, bank):
            k = cnt["i"]
            cnt["i"] += 1
            head_epilogue(bank, c, R, heI, onesblk_bf, 1.0 / 64, CC("g_bq"), CC("PI"),
                          qA_d[m * 128:(m + 1) * 128, chunk(c)], ("qB", m, c), 4 + k % 2, 6 + k % 2)

        gemm("b_q", range(KD), KD, lambda kc, c: nT[:, kc, chunk(c)], lambda kc, c: ("nT", kc, c), epi, [0, 1, 2, 3])
        P.barrier()
        AR.release(m0)

    def attnB_core():
        m0 = AR.mark()
        kdup = AR.alloc((4, TP), BF16)
        vT = AR.alloc((2, TP), BF16)
        vtok = AR.alloc((NT, 256), BF16)
        esink = AR.alloc((16,), F32)
        es = Ring(AR, 3, (512,), BF16, "es")
        pms = Ring(AR, 3, (512,), BF16, "pm")
        qts = Ring(AR, 2, (16, 128), BF16, "qt")
        osts = Ring(AR, 2, (16, 128), BF16, "ost")
        rdens = Ring(AR, 2, (512,), F32, "rden")
        kshv = ksh_d.rearrange("(n p) t -> p n t", p=64)
        P.add("act", lambda e: e.dma_start(out=kdup[0:64, :, :], in_=kshv), writes=["kdup0"], dma_sem="bigA")
        P.add("act", lambda e: e.dma_start(out=kdup[64:128, :, :], in_=kshv), writes=["kdup1"], dma_sem="bigB")
        P.add("act", lambda e: e.dma_start(out=vT, in_=vsh_d.rearrange("(c p) t -> p c t", p=128)), writes=["vT"], dma_sem="bigC")
        P.add("act", lambda e: e.activation(out=esink, in_=CC("sinkT"), func=AF.Exp), reads=CK, writes=["esink"])
        for j in range(NT):
            for n in range(2):
                P.add("pe", lambda e, o=PS(2).bitcast(BF16)[:, n * 128:(n + 1) * 128], a=vT[:, n, 128 * j:128 * (j + 1)]:
                      e.transpose(out=o, in_=a, identity=ident_bf), reads=["vT"] + CK, writes=[PK(2)])
            P.add("act", lambda e, o=vtok[:, j, :], i=PS(2).bitcast(BF16)[:, 0:256]: e.activation(out=o, in_=i, func=AF.Copy),
                  reads=[PK(2)], writes=[("vtok", j)])
        sbc = 0
        for i in range(NT):
            qa, qk, qsem = qts.next()
            P.add("act", lambda e, o=qa, s=qA_d.rearrange("(h p) t -> p h t", p=128)[:, :, 128 * i:128 * (i + 1)]:
                  e.dma_start(out=o, in_=s), reads=[("qB",)], writes=[qk], dma_sem=qsem)
            oa, ok, osem = osts.next()
            js = [j for j in (i - 1, i) if j >= 0]
            for n in range(4):
                for ev in range(2):
                    b0 = 64 * ev
                    for j in js:
                        sb = 3 + (sbc % 2)
                        sbc += 1
                        P.add("pe", lambda e, o=PS(sb).rearrange("p (a b) -> p a b", b=128), l=kdup[b0:b0 + 64, n, 128 * j:128 * (j + 1)], r=qa[b0:b0 + 64, 4 * n:4 * n + 4, :]:
                              e.matmul(o, l, r, start=True, stop=True), reads=["kdup%d" % ev, qk], writes=[PK(sb)])
                        ea, ek, _ = es.next()
                        P.add("act", lambda e, o=ea, s=PS(sb): e.activation(out=o, in_=s, func=AF.Exp, scale=0.125),
                              reads=[PK(sb)], writes=[ek])
                        pa, pk, _ = pms.next()
                        msk = mdiagT_bf if j == i else mprevT_bf
                        P.add("pool", lambda e, o=pa.rearrange("p (a b) -> p a b", b=128), a=ea.rearrange("p (a b) -> p a b", b=128),
                              m=msk.rearrange("p (a b) -> p a b", a=1).to_broadcast([128, 4, 128]):
                              e.tensor_tensor(out=o, in0=a, in1=m, op=ALU.mult), reads=[ek] + CK, writes=[pk])
                        P.add("pe", lambda e, o=PS(5)[b0:b0 + 64, :], l=vtok[:, j, n * 64:(n + 1) * 64], r=pa, st_=(j == js[0]), sp_=(j == js[-1]):
                              e.matmul(o, l, r, start=st_, stop=sp_),
                              reads=[("vtok", j), pk], writes=[PK(5)])
                        P.add("pe", lambda e, o=PS(6)[b0:b0 + 64, :], l=ones_bf[:, 0:64], r=pa, st_=(j == js[0]), sp_=(j == js[-1]):
                              e.matmul(o, l, r, start=st_, stop=sp_),
                              reads=[pk] + CK, writes=[PK(6)])
                ra, rk, _ = rdens.next()
                r3 = ra.rearrange("p (a b) -> p a b", b=128)
                P.add("dve", lambda e, o=r3, d=PS(6).rearrange("p (a b) -> p a b", b=128),
                      s=esink[:, 4 * n:4 * n + 4].rearrange("p (a b) -> p a b", b=1).to_broadcast([128, 4, 128]):
                      e.tensor_tensor(out=o, in0=d, in1=s, op=ALU.add), reads=[PK(6), "esink"], writes=[rk])
                P.add("dve", lambda e, o=ra: e.reciprocal(out=o, in_=o), reads=[rk], writes=[rk])
                P.add("dve", lambda e, o=oa[:, 4 * n:4 * n + 4, :], r=r3:
                      e.tensor_tensor(out=o, in0=PS(5).rearrange("p (a b) -> p a b", b=128), in1=r, op=ALU.mult),
                      reads=[PK(5), rk], writes=[ok])
            P.add("act", lambda e, s=oa, o=oT_d.rearrange("(h p) t -> p h t", p=128)[:, :, 128 * i:128 * (i + 1)]:
                  e.dma_start(out=o, in_=s), reads=[ok], writes=[("oT", i)], dma_sem=osem)
        P.barrier()
        AR.release(m0)

    def final_out():
        m0 = AR.mark()
        hts = Ring(AR, 2, (KD, 128), F32, "ht")
        ors = Ring(AR, 2, (D,), F32, "orow")
        for r in range(16):
            tok0 = NMETA + 128 * r
            ha, hk, hsem = hts.next()
            P.add("act", lambda e, o=ha, s=hT_v[:, :, tok0:tok0 + 128]: e.dma_start(out=o, in_=s), writes=[hk], dma_sem=hsem)
            oa, ok, osem = ors.next()
            for q4 in range(4):
                bank = q4 % 2
                for rr_ in range(4):
                    kc = q4 * 4 + rr_
                    P.add("pe", lambda e, o=PS(bank)[:, rr_ * 128:(rr_ + 1) * 128], a=ha[:, kc, :]:
                          e.transpose(out=o, in_=a, identity=CC("ident")), reads=[hk] + CK, writes=[PK(bank)])
                if q4 % 2:
                    P.add("dve", lambda e, o=oa[:, q4 * 512:(q4 + 1) * 512], s=PS(bank): e.tensor_copy(out=o, in_=s),
                          reads=[PK(bank)], writes=[ok])
                else:
                    P.add("act", lambda e, o=oa[:, q4 * 512:(q4 + 1) * 512], s=PS(bank): e.activation(out=o, in_=s, func=AF.Copy),
                          reads=[PK(bank)], writes=[ok])
            P.add("act", lambda e, o=out_d[128 * r:128 * (r + 1), :], s=oa: e.dma_start(out=o, in_=s),
                  reads=[ok], writes=[("out", r)], dma_sem=osem)
        P.barrier()
        AR.release(m0)

    wiT = AR.alloc((TP,), F32)
    P.add("pool", lambda e: e.memset(wiT, 0.0), writes=["wiT_all"])
    for ph in ALL_PHASES:
        if ph not in phases:
            continue
        if ph == "f01":
            ffn(0, 1)
        elif ph == "ain":
            attnA_inproj(wiT)
        elif ph == "acore":
            attnA_core(wiT)
        elif ph == "aout":
            outproj("a_out", oT_d)
        elif ph == "f02":
            ffn(0, 2)
        elif ph == "kv":
            kv_proj()
        elif ph == "f11":
            ffn(1, 1)
        elif ph == "bq":
            attnB_qproj()
        elif ph == "bcore":
            attnB_core()
        elif ph == "bout":
            outproj("b_out", oT_d)
        elif ph == "f12":
            ffn(1, 2)
    final_out()
    if debug:
        P.add("act", lambda e: e.dma_start(out=dbg_d, in_=hT_d), writes=["dbg"], dma_sem="dbg")
        for nm_, src_ in [("dbg_oT", oT_d), ("dbg_q", qA_d), ("dbg_ksh", ksh_d), ("dbg_vsh", vsh_d)]:
            dd_ = nc.dram_tensor(nm_, list(src_.shape), BF16, kind="ExternalOutput").ap()
            P.add("act", lambda e, o=dd_, i=src_: e.dma_start(out=o, in_=i), writes=[nm_], dma_sem=nm_)
    P.barrier()
    P.add("act", lambda e: e.nop(), reads=[], writes=["fin"])

    with nc.Block() as block:
        P.emit(nc, block, stack)
    nc._keepalive_stack = stack
    return nc


_NC_CACHE = {}


def kernel(**inp):
    inp = {k: np.asarray(v) for k, v in inp.items()}
    cp, tabs = build_consts(inp)
    W = prep_weights(inp)
    if "nc" not in _NC_CACHE:
        _NC_CACHE["nc"] = build_nc()
    nc = _NC_CACHE["nc"]
    shared = {"meta": np.ascontiguousarray(inp["meta_tokens"], dtype=np.float32), "cpack": cp, "tabs": tabs}
    for k, v in W.items():
        shared["w_" + k] = v
    in_maps = []
    for b in range(8):
        m = dict(shared)
        m["x"] = np.ascontiguousarray(inp["x"][b], dtype=np.float32)
        in_maps.append(m)
    res = run_bass_kernel_spmd(nc, in_maps, core_ids=list(range(8)))
    return np.stack([np.asarray(r["out"], dtype=np.float32) for r in res.results], axis=0)
```
